# Optimizing a Trainium2 kernel written in Bass

```python
import math
import jax, jax.numpy as jnp
from jax import lax
import numpy as np

D_MODEL = 1024
BATCH = 16
SEQ = 2048
DEPTH = 2

HEAD_DIM = 64
N_HEADS_FOX = 8
N_HEADS_SB = 8
DIL_PATTERNS = ((128, 1), (512, 4), (2048, 16))
N_DIL_GROUPS = len(DIL_PATTERNS)
N_HEADS_DIL = 4
D_FF = 4 * D_MODEL
ROPE_THETA = 10000.0
Q_BLOCK = 128
EPS = 1e-6
N_BRANCHES = 3

W_FOX = N_HEADS_FOX * HEAD_DIM
W_SB = N_HEADS_SB * HEAD_DIM
W_DIL = N_HEADS_DIL * HEAD_DIM
SZ_FOX_QKV = 3 * W_FOX
SZ_FORGET = N_HEADS_FOX
SZ_SB_QKV = 3 * W_SB
SZ_DIL_QKV = 3 * N_DIL_GROUPS * W_DIL
SZ_GATES = N_BRANCHES * D_MODEL
D_IN = SZ_FOX_QKV + SZ_FORGET + SZ_SB_QKV + SZ_DIL_QKV + SZ_GATES

kernel_name = "gated_parallel_fox_stickbreak_dilated"


def rms_norm(x, g):
    xf = x.astype(jnp.float32)
    y = xf * lax.rsqrt(jnp.mean(xf * xf, axis=-1, keepdims=True) + EPS)
    return (y * g.astype(jnp.float32)).astype(x.dtype)


def rope(x, positions):
    half = x.shape[-1] // 2
    inv = 1.0 / (ROPE_THETA ** (jnp.arange(half, dtype=jnp.float32) / half))
    ang = positions.astype(jnp.float32)[..., None] * inv
    cos = jnp.cos(ang)[:, :, None, :]
    sin = jnp.sin(ang)[:, :, None, :]
    xf = x.astype(jnp.float32)
    x1, x2 = xf[..., :half], xf[..., half:]
    return jnp.concatenate([x1 * cos - x2 * sin, x2 * cos + x1 * sin], axis=-1).astype(x.dtype)


def forgetting_attention(q, k, v, f_cum):
    B, T, H, D = q.shape
    scale = 1.0 / math.sqrt(D)
    f_t = jnp.transpose(f_cum, (0, 2, 1))
    outs = []
    for i in range(T // Q_BLOCK):
        lo, hi = i * Q_BLOCK, (i + 1) * Q_BLOCK
        s = jnp.einsum('bqhd,bkhd->bhqk', q[:, lo:hi], k[:, :hi]).astype(jnp.float32) * scale
        s = s + f_t[:, :, lo:hi, None] - f_t[:, :, None, :hi]
        mask = (lo + np.arange(Q_BLOCK))[:, None] >= np.arange(hi)[None, :]
        s = jnp.where(mask, s, -jnp.inf)
        p = jax.nn.softmax(s, axis=-1).astype(v.dtype)
        outs.append(jnp.einsum('bhqk,bkhd->bqhd', p, v[:, :hi]))
    return jnp.concatenate(outs, axis=1)


def stick_breaking_attention(q, k, v):
    B, T, H, D = q.shape
    scale = 1.0 / math.sqrt(D)
    outs = []
    for i in range(T // Q_BLOCK):
        lo, hi = i * Q_BLOCK, (i + 1) * Q_BLOCK
        z = jnp.einsum('bqhd,bkhd->bhqk', q[:, lo:hi], k[:, :hi]).astype(jnp.float32) * scale
        mask = (lo + np.arange(Q_BLOCK))[:, None] > np.arange(hi)[None, :]
        log_not = jnp.where(mask, jax.nn.log_sigmoid(-z), 0.0)
        later = lax.cumsum(log_not, axis=3, reverse=True) - log_not
        a = jnp.where(mask, jnp.exp(jax.nn.log_sigmoid(z) + later), 0.0)
        outs.append(jnp.einsum('bhqk,bkhd->bqhd', a.astype(v.dtype), v[:, :hi]))
    return jnp.concatenate(outs, axis=1)


def dilated_window_attention(q, k, v, window, dilation):
    B, T, H, D = q.shape
    n = T // dilation
    W = window // dilation
    Z = B * dilation
    scale = 1.0 / math.sqrt(D)

    def to_streams(a):
        return a.reshape(B, n, dilation, H, D).transpose(0, 2, 1, 3, 4).reshape(Z, n, H, D)

    qs, ks, vs = to_streams(q), to_streams(k), to_streams(v)
    qb = math.gcd(n, Q_BLOCK)
    nb = n // qb
    pad = ((0, 0), (W, 0), (0, 0), (0, 0))
    kp, vp = jnp.pad(ks, pad), jnp.pad(vs, pad)
    idx = np.arange(nb)[:, None] * qb + np.arange(qb + W)[None, :]
    kblk, vblk = kp[:, idx], vp[:, idx]
    qblk = qs.reshape(Z, nb, qb, H, D)
    s = jnp.einsum('znqhd,znkhd->znhqk', qblk, kblk).astype(jnp.float32) * scale
    dist = np.arange(qb)[:, None] + W - np.arange(qb + W)[None, :]
    band = (dist >= 0) & (dist <= W)
    mask = band[None] & (idx - W >= 0)[:, None, :]
    s = jnp.where(mask[None, :, None], s, -jnp.inf)
    m = jnp.max(s, axis=-1, keepdims=True)
    p = jnp.exp(s - m)
    den = jnp.sum(p, axis=-1, keepdims=True)
    o = jnp.einsum('znhqk,znkhd->znqhd', (p / den).astype(v.dtype), vblk)
    lse = jnp.transpose((m + jnp.log(den))[..., 0], (0, 1, 3, 2))
    o = o.reshape(B, dilation, n, H, D).transpose(0, 2, 1, 3, 4).reshape(B, T, H, D)
    lse = lse.reshape(B, dilation, n, H).transpose(0, 2, 1, 3).reshape(B, T, H)
    return o, lse


def setup_inputs(seed: int = 0) -> dict:
    key = jax.random.key(seed)
    ks = jax.random.split(key, 17)
    f32 = jnp.float32

    def nrm(k, shape, fan_in, mult=1.0):
        return jax.random.normal(k, shape, f32) * (mult * fan_in ** -0.5)

    def gain(k, shape):
        return 1.0 + 0.05 * jax.random.normal(k, shape, f32)

    x = jax.random.normal(ks[0], (BATCH, SEQ, D_MODEL), f32)
    positions = jnp.broadcast_to(jnp.arange(SEQ, dtype=jnp.int32), (BATCH, SEQ))
    return {
        "x": x,
        "positions": positions,
        "attn_norm": gain(ks[1], (DEPTH, D_MODEL)),
        "w_in": nrm(ks[2], (DEPTH, D_MODEL, D_IN), D_MODEL),
        "b_forget": 2.0 + 0.1 * jax.random.normal(ks[3], (DEPTH, N_HEADS_FOX), f32),
        "q_norm_fox": gain(ks[4], (DEPTH, HEAD_DIM)),
        "k_norm_fox": gain(ks[5], (DEPTH, HEAD_DIM)),
        "q_norm_dil": gain(ks[6], (DEPTH, HEAD_DIM)),
        "k_norm_dil": gain(ks[7], (DEPTH, HEAD_DIM)),
        "w_up_fox": nrm(ks[8], (DEPTH, W_FOX, D_MODEL), W_FOX),
        "w_up_sb": nrm(ks[9], (DEPTH, W_SB, D_MODEL), W_SB),
        "w_up_dil": nrm(ks[10], (DEPTH, W_DIL, D_MODEL), W_DIL),
        "w_out": nrm(ks[11], (DEPTH, D_MODEL, D_MODEL), D_MODEL),
        "mlp_norm": gain(ks[12], (DEPTH, D_MODEL)),
        "w_mlp_in": nrm(ks[13], (DEPTH, D_MODEL, D_FF), D_MODEL),
        "w_mlp_out": nrm(ks[14], (DEPTH, D_FF, D_MODEL), D_FF, 0.5),
    }


def reference(x, positions, attn_norm, w_in, b_forget, q_norm_fox, k_norm_fox, q_norm_dil, k_norm_dil,
              w_up_fox, w_up_sb, w_up_dil, w_out, mlp_norm, w_mlp_in, w_mlp_out):
    B, T, _ = x.shape
    o1 = SZ_FOX_QKV
    o2 = o1 + SZ_FORGET
    o3 = o2 + SZ_SB_QKV
    o4 = o3 + SZ_DIL_QKV
    for l in range(DEPTH):
        h = rms_norm(x, attn_norm[l])
        proj = h @ w_in[l]

        fox = proj[..., :o1].reshape(B, T, 3, N_HEADS_FOX, HEAD_DIM)
        qa = rms_norm(fox[:, :, 0], q_norm_fox[l])
        ka = rms_norm(fox[:, :, 1], k_norm_fox[l])
        va = fox[:, :, 2]
        log_f = jax.nn.log_sigmoid(proj[..., o1:o2].astype(jnp.float32) + b_forget[l].astype(jnp.float32))
        f_cum = jnp.cumsum(log_f, axis=1)
        out_a = forgetting_attention(qa, ka, va, f_cum)

        sb = proj[..., o2:o3].reshape(B, T, 3, N_HEADS_SB, HEAD_DIM)
        out_b = stick_breaking_attention(sb[:, :, 0], sb[:, :, 1], sb[:, :, 2])

        dil = proj[..., o3:o4].reshape(B, T, 3, N_DIL_GROUPS * N_HEADS_DIL, HEAD_DIM)
        qc = rope(rms_norm(dil[:, :, 0], q_norm_dil[l]), positions)
        kc = rope(rms_norm(dil[:, :, 1], k_norm_dil[l]), positions)
        vc = dil[:, :, 2]
        group_o, group_lse = [], []
        for g, (window, dilation) in enumerate(DIL_PATTERNS):
            sl = slice(g * N_HEADS_DIL, (g + 1) * N_HEADS_DIL)
            o_g, lse_g = dilated_window_attention(qc[:, :, sl], kc[:, :, sl], vc[:, :, sl], window, dilation)
            group_o.append(o_g)
            group_lse.append(lse_g)
        wts = jax.nn.softmax(jnp.stack(group_lse, axis=0), axis=0)
        out_c = jnp.sum(wts[..., None].astype(vc.dtype) * jnp.stack(group_o, axis=0), axis=0)

        gates = jax.nn.sigmoid(proj[..., o4:].astype(jnp.float32)).astype(x.dtype).reshape(B, T, N_BRANCHES, D_MODEL)
        y_a = out_a.reshape(B, T, W_FOX) @ w_up_fox[l]
        y_b = out_b.reshape(B, T, W_SB) @ w_up_sb[l]
        y_c = out_c.reshape(B, T, W_DIL) @ w_up_dil[l]
        merged = gates[:, :, 0] * y_a + gates[:, :, 1] * y_b + gates[:, :, 2] * y_c
        x = x + merged @ w_out[l]

        h2 = rms_norm(x, mlp_norm[l])
        x = x + jnp.square(jax.nn.relu(h2 @ w_mlp_in[l])) @ w_mlp_out[l]
    return x
```

```python
import contextlib
import math
import numpy as np
import concourse.bass as bass
import concourse.mybir as mybir
from concourse.bass_utils import run_bass_kernel_spmd

F32 = mybir.dt.float32
BF16 = mybir.dt.bfloat16
I32 = mybir.dt.int32
AF = mybir.ActivationFunctionType
ALU = mybir.AluOpType

T = 2048
D = 1024
NT = 16
NG = 4
KC = 8
DIN = 8456
DFF = 4096
O_FQ, O_FK, O_FV, O_FF = 0, 512, 1024, 1536
O_SQ, O_SK, O_SV = 1544, 2056, 2568
O_DQ, O_DK, O_DV = 3080, 3848, 4616
O_G = 5384
EPS = 1e-6
DILS = (1, 4, 16)
WARM_SB, WARM_FOX, WARM_DIL = 0, 0, 0
DUP_FOX = 0

CB_ID, CB_TRII, CB_TRIC, CB_MGE, CB_MLE, CB_MGT, CB_ONES = 0, 128, 256, 384, 512, 640, 768
CB_N = 768 + 512
CF_BLK, CF_PERM, CF_INV, CF_NEGC, CF_NEGD = 0, 128, 256, 257, 258
CF_N = 260


def host_consts():
    p = np.arange(128)
    cb = np.zeros((128, CB_N), np.float32)
    cb[:, CB_ID:CB_ID + 128] = np.eye(128)
    cb[:, CB_TRII:CB_TRII + 128] = (p[:, None] >= p[None, :])
    cb[:, CB_TRIC:CB_TRIC + 128] = (p[:, None] < p[None, :])
    cb[:, CB_MGE:CB_MGE + 128] = (p[None, :] >= p[:, None])
    cb[:, CB_MGT:CB_MGT + 128] = (p[None, :] > p[:, None])
    cb[:, CB_MLE:CB_MLE + 128] = (p[None, :] <= p[:, None])
    cb[:, CB_ONES:] = 1.0
    cf = np.zeros((128, CF_N), np.float32)
    cf[:, CF_BLK:CF_BLK + 128] = (p[:, None] // 64 == p[None, :] // 64)
    perm = np.zeros((128, 128), np.float32)
    for m in range(128):
        if m % 64 < 32:
            perm[m + 32, m] = -1.0
        else:
            perm[m - 32, m] = 1.0
    cf[:, CF_PERM:CF_PERM + 128] = perm
    cf[:, CF_INV] = (1.0 / (10000.0 ** ((p % 32).astype(np.float32) / np.float32(32.0)))).astype(np.float32)
    c = np.where(p % 32 == 0, 0.0, 1.0)
    d = np.where(p % 32 == 2, 1.0, 0.0)
    cf[:, CF_NEGC] = -c
    cf[:, CF_NEGD] = -d
    return cb, cf


class Buf:
    __slots__ = ("w", "r")

    def __init__(self):
        self.w = None
        self.r = {}


class Builder:
    def __init__(self, nc, es):
        self.nc = nc
        self.es = es
        self.E = {"pe": nc.tensor, "act": nc.scalar, "dve": nc.vector, "pool": nc.gpsimd, "sp": nc.sync}
        self.sem = {k: es.enter_context(nc.semaphore("s_" + k)) for k in ("pe", "act", "dve", "pool")}
        self.cnt = {k: 0 for k in self.sem}
        self.obs = {k: {} for k in self.E}
        self.dsems = []
        self.pe_pending = False
        self.uid = 0

    def new_dsem(self):
        h = self.es.enter_context(self.nc.semaphore("d%d" % len(self.dsems)))
        ds = ["d%d" % len(self.dsems), h, 0, None]
        self.dsems.append(ds)
        return ds

    def _wait(self, eng, tok):
        if tok is None:
            return
        key, h, v = tok
        if eng == "pe" and key == "pe":
            return
        if self.obs[eng].get(key, 0) >= v:
            return
        self.E[eng].wait_ge(h, v)
        self.obs[eng][key] = v

    def _deps(self, eng, reads, writes):
        for b in reads:
            self._wait(eng, b.w)
        for b in writes:
            self._wait(eng, b.w)
            for tok in list(b.r.values()):
                self._wait(eng, tok)

    def op(self, eng, fn, reads=(), writes=(), signal=True):
        self._deps(eng, reads, writes)
        ins = fn()
        tick = self.cnt[eng] + 1
        if signal:
            ins.then_inc(self.sem[eng], 1)
            self.cnt[eng] = tick
            if eng == "pe":
                self.pe_pending = False
        elif eng == "pe":
            self.pe_pending = True
        tok = (eng, self.sem[eng], tick)
        for b in reads:
            b.r[eng] = tok
        for b in writes:
            b.w = tok
            b.r = {}
        return ins

    def dma(self, q, out, in_, ds, reads=(), writes=(), **kw):
        gk = id(writes[0]) if writes else id(reads[0])
        if ds[3] is not None and ds[3] != gk and ds[2] > 0:
            self._wait(q, (ds[0], ds[1], ds[2]))
        ds[3] = gk
        self._deps(q, reads, writes)
        ins = self.E[q].dma_start(out=out, in_=in_, **kw)
        ds[2] += 16
        ins.then_inc(ds[1], 16)
        tok = (ds[0], ds[1], ds[2])
        for b in reads:
            b.r[ds[0]] = tok
        for b in writes:
            b.w = tok
            b.r = {}

    def barrier(self):
        assert not self.pe_pending
        for e in self.E:
            for k in self.sem:
                if k != e and self.cnt[k] > 0:
                    self._wait(e, (k, self.sem[k], self.cnt[k]))
            for ds in self.dsems:
                if ds[2] > 0:
                    self._wait(e, (ds[0], ds[1], ds[2]))

    def sb(self, scope, shape, dt, name=None):
        self.uid += 1
        return scope.enter_context(self.nc.sbuf_tensor("%s_%d" % (name or "t", self.uid), list(shape), dt))

    def mm(self, out, lhsT, rhs, start, stop, reads, writes, signal=None):
        if signal is None:
            signal = stop
        return self.op("pe", lambda: self.nc.tensor.matmul(out, lhsT=lhsT, rhs=rhs, start=start, stop=stop,
                                                             skip_group_check=True),
                       reads=reads, writes=writes, signal=signal)

    def act(self, out, in_, func, reads, writes, scale=None, bias=None, accum_out=None):
        kw = {}
        if scale is not None:
            kw["scale"] = scale
        if bias is not None:
            kw["bias"] = bias
        if accum_out is not None:
            kw["accum_out"] = accum_out
        return self.op("act", lambda: self.nc.scalar.activation(out=out, in_=in_, func=func, **kw), reads=reads, writes=writes)

    def tt(self, out, in0, in1, op, reads, writes, eng="dve"):
        e = self.E[eng]
        return self.op(eng, lambda: e.tensor_tensor(out=out, in0=in0, in1=in1, op=op), reads=reads, writes=writes)

    def stt(self, out, in0, scalar, in1, op0, op1, reads, writes):
        return self.op("dve", lambda: self.nc.vector.scalar_tensor_tensor(out=out, in0=in0, scalar=scalar, in1=in1, op0=op0, op1=op1),
                       reads=reads, writes=writes)

    def ts(self, out, in0, s1, op0, reads, writes, s2=None, op1=None, eng="dve"):
        e = self.E[eng]
        if op1 is None:
            return self.op(eng, lambda: e.tensor_scalar(out=out, in0=in0, scalar1=s1, scalar2=None, op0=op0), reads=reads, writes=writes)
        return self.op(eng, lambda: e.tensor_scalar(out=out, in0=in0, scalar1=s1, scalar2=s2, op0=op0, op1=op1), reads=reads, writes=writes)

    def cp(self, out, in_, reads, writes, eng="dve"):
        e = self.E[eng]
        return self.op(eng, lambda: e.tensor_copy(out=out, in_=in_), reads=reads, writes=writes)

    def memset(self, ap, val, writes, eng="dve"):
        e = self.E[eng]
        return self.op(eng, lambda: e.memset(ap, val), reads=(), writes=writes)


class Rot:
    def __init__(self, items):
        self.items = items
        self.i = 0

    def next(self):
        it = self.items[self.i % len(self.items)]
        self.i += 1
        return it


def build_program(nseq=2, nlayer=2, dbg=None):
    nc = bass.Bass("TRN2", target_bir_lowering=False)
    dt_in = {}

    def din(name, shape, dt=F32):
        dt_in[name] = nc.dram_tensor(name, list(shape), dt, kind="ExternalInput").ap()
        return dt_in[name]

    x_d = din("x", [2, T, D])
    pos_d = din("positions", [2, T], I32)
    attn_norm_d = din("attn_norm", [2, D])
    w_in_d = din("w_in", [2, D, DIN])
    b_forget_d = din("b_forget", [2, 8])
    qnf_d = din("q_norm_fox", [2, 64])
    knf_d = din("k_norm_fox", [2, 64])
    qnd_d = din("q_norm_dil", [2, 64])
    knd_d = din("k_norm_dil", [2, 64])
    wuf_d = din("w_up_fox", [2, 512, D])
    wus_d = din("w_up_sb", [2, 512, D])
    wud_d = din("w_up_dil", [2, 256, D])
    wout_d = din("w_out", [2, D, D])
    mlp_norm_d = din("mlp_norm", [2, D])
    w1_d = din("w_mlp_in", [2, D, DFF])
    w2_d = din("w_mlp_out", [2, DFF, D])
    cb_d = din("cst_b", [128, CB_N])
    cf_d = din("cst_f", [128, CF_N])
    out_d = nc.dram_tensor("out", [2, T, D], F32, kind="ExternalOutput").ap()
    xmid_d = nc.dram_tensor("xmid", [2, T, D], F32, kind="Internal").ap()
    dbg_d = {}
    if dbg:
        for name, shape in dbg.items():
            dbg_d[name] = nc.dram_tensor("dbg_" + name, list(shape), F32, kind="ExternalOutput").ap()

    with contextlib.ExitStack() as es:
        B = Builder(nc, es)
        PS = []
        for i in range(8):
            t = es.enter_context(nc.psum_tensor("ps%d" % i, [128, 512], F32))
            PS.append((t, Buf()))
        cb = B.sb(es, [128, CB_N], BF16, "cb")
        cf = B.sb(es, [128, CF_N], F32, "cf")
        cbb, cfb = Buf(), Buf()
        ds_c = B.new_dsem()
        ds_c2 = B.new_dsem()
        B.dma("pool", cb[:], cb_d[:, :], ds_c, writes=[cbb])
        B.dma("sp", cf[:], cf_d[:, :], ds_c2, writes=[cfb])
        ident = cb[:, CB_ID:CB_ID + 128]
        triI = cb[:, CB_TRII:CB_TRII + 128]
        triC = cb[:, CB_TRIC:CB_TRIC + 128]
        mge = cb[:, CB_MGE:CB_MGE + 128]
        mgt = cb[:, CB_MGT:CB_MGT + 128]
        mle = cb[:, CB_MLE:CB_MLE + 128]
        mband = cb[:, CB_MGE:CB_MGE + 256]
        ones_b = cb[:, CB_ONES:CB_ONES + 512]
        blk = cf[:, CF_BLK:CF_BLK + 128]
        perm = cf[:, CF_PERM:CF_PERM + 128]
        inv_c = cf[:, CF_INV:CF_INV + 1]
        negc = cf[:, CF_NEGC:CF_NEGC + 1]
        negd = cf[:, CF_NEGD:CF_NEGD + 1]

        ds_w = [B.new_dsem() for _ in range(6)]
        ds_x = [B.new_dsem() for _ in range(4)]
        ds_small = B.new_dsem()
        ds_small_pool = B.new_dsem()
        ds_outs = [B.new_dsem() for _ in range(4)]
        ds_dbg = B.new_dsem()
        wsem_i = [0]

        def wsem():
            wsem_i[0] += 1
            return ds_w[wsem_i[0] % len(ds_w)]

        def tap(name, ap, bufs):
            if dbg and name in dbg_d:
                B.dma("sp", dbg_d[name], ap, ds_dbg, reads=bufs)

        def pipeline_gen(items, stages, lag=1):
            n, S = len(items), len(stages)
            for it in range(n + lag * (S - 1)):
                for k in range(S - 1, -1, -1):
                    i = it - k * lag
                    if 0 <= i < n:
                        stages[k](items[i])
                yield

        def pipeline_n(items, stages, lag=1):
            n, S = len(items), len(stages)
            for it in range(n + lag * (S - 1)):
                for k in range(S - 1, -1, -1):
                    i = it - k * lag
                    if 0 <= i < n:
                        stages[k](items[i])

        def norm_phase(scope, get_x, gain_d_row, hT, hTb):
            gbc = B.sb(scope, [128, D], F32, "gbc")
            gbcb = Buf()
            B.dma("sp", gbc[:], gain_d_row.partition_broadcast(128), ds_small, writes=[gbcb])
            ss = B.sb(scope, [128, NT], F32, "ss")
            l1 = B.sb(scope, [128, NT], F32, "l1")
            rstd = B.sb(scope, [128, NT], F32, "rstd")
            ssb = [Buf() for _ in range(NT)]
            junk = B.sb(scope, [128, D], BF16, "junk")
            junkb = Buf()
            hn = Rot([(B.sb(scope, [128, D], BF16, "hn"), Buf()) for _ in range(3)])
            pst = Rot([PS[0], PS[1], PS[2]])
            items = [dict(t=t) for t in range(NT)]

            def n0(it):
                t = it["t"]
                xa, xb = get_x(t)
                it["x"] = (xa, xb)
                B.act(junk[:], xa, AF.Square, reads=[xb], writes=[junkb, ssb[t]], accum_out=ss[:, t:t + 1])
                B.act(l1[:, t:t + 1], ss[:, t:t + 1], AF.Ln, reads=[ssb[t]], writes=[ssb[t]], scale=1.0 / D, bias=EPS)
                B.act(rstd[:, t:t + 1], l1[:, t:t + 1], AF.Exp, reads=[ssb[t]], writes=[ssb[t]], scale=-0.5)

            def n1(it):
                t = it["t"]
                xa, xb = it["x"]
                h, hb = hn.next()
                B.stt(h[:], xa, rstd[:, t:t + 1], gbc[:], ALU.mult, ALU.mult, reads=[xb, ssb[t], gbcb], writes=[hb])
                it["h"] = (h, hb)

            def n2(it):
                h, hb = it["h"]
                p, pb = pst.next()
                pv = p[:].bitcast(BF16).rearrange("p (k t) -> p k t", k=KC)
                for kc in range(KC):
                    B.op("pe", lambda kc=kc: nc.tensor.transpose(out=pv[:, kc, :], in_=h[:, kc * 128:(kc + 1) * 128], identity=ident),
                         reads=[hb, cbb], writes=[pb], signal=(kc == KC - 1))
                it["p"] = (pv, pb)

            def n3(it):
                t = it["t"]
                pv, pb = it["p"]
                B.op("act", lambda: nc.scalar.activation(out=hT[:, 0:4, t * 128:(t + 1) * 128], in_=pv[:, 0:4, :], func=AF.Copy), reads=[pb], writes=[hTb[t // 4]])
                B.cp(hT[:, 4:8, t * 128:(t + 1) * 128], pv[:, 4:8, :], reads=[pb], writes=[hTb[t // 4]])
            pipeline_n(items, [n0, n1, n2, n3])

        def load_w3(W3, W3b, wl, c_q, c_k, c_v):
            src = wl.rearrange("(kc p) n -> p kc n", p=128)
            ds = wsem()
            for i, c0 in enumerate((c_q, c_k, c_v)):
                B.dma("pool", W3[:, :, i * 128:(i + 1) * 128], src[:, :, c0:c0 + 128], ds, writes=[W3b])

        def proj_fm(bank, W, c0, M, g, reads, hT, hTb):
            p, pb = bank
            for kc in range(KC):
                B.mm(p[0:M, :], W[:, kc, c0:c0 + M], hT[:, kc, g * 512:(g + 1) * 512], kc == 0, kc == KC - 1,
                     reads=reads + [hTb[g]], writes=[pb])

        def proj_v(banks, W3, W3b, V2, V2b, hT, hTb, tile_cols):
            for t4 in range(0, NT, 4):
                p, pb = banks.next()
                for tt_ in range(4):
                    t = t4 + tt_
                    for kc in range(KC):
                        B.mm(p[:, tt_ * 128:(tt_ + 1) * 128], hT[:, kc, tile_cols(t)], W3[:, kc, 256:384], kc == 0, kc == KC - 1,
                             reads=[W3b] + hTb, writes=[pb], signal=(kc == KC - 1 and tt_ == 3))
                B.cp(V2[:, t4:t4 + 4, :, 0:64], p[:].rearrange("p (t h d) -> p t h d", t=4, h=2), reads=[pb], writes=[V2b])

        def pipeline(tiles, s1, s2, look=2):
            n = len(tiles)
            for i in range(n + look):
                if i < n:
                    s1(tiles[i])
                if i - look >= 0:
                    s2(tiles[i - look])

        def attention_phase(scope, l, s, hT, hTb, OT, OTb, MU):
            wl = w_in_d[l]
            acc = MU[:, 0:4, :].bitcast(F32).rearrange("p a n -> p (a n)").rearrange("p (s t) -> p s t", s=2)
            accb = [Buf(), Buf()]
            cosT = MU[:, 4:6, :].bitcast(F32).rearrange("p a n -> p (a n)")
            sinT = MU[:, 6:8, :].bitcast(F32).rearrange("p a n -> p (a n)")
            csb = Buf()
            W3r = Rot([(B.sb(scope, [128, KC, 384], BF16, "W3"), Buf()) for _ in range(2)])
            _w0 = W3r.next()
            load_w3(_w0[0], _w0[1], wl, O_SQ, O_SK, O_SV)
            with contextlib.ExitStack() as R:
                posi = B.sb(R, [128, T], I32, "posi")
                posb = Buf()
                B.dma("sp", posi[:], pos_d[s:s + 1, :].partition_broadcast(128), ds_small, writes=[posb])
                ya = acc[:, 0, :]
                yb_ = acc[:, 1, :]
                yab, ybb = accb[0], accb[1]
                B.cp(ya, posi[:], reads=[posb], writes=[yab])
                B.ts(ya, ya, inv_c, ALU.mult, reads=[yab, cfb], writes=[yab])
                B.ts(ya, ya, float(1.0 / (2.0 * math.pi)), ALU.mult, reads=[yab], writes=[yab])
                for dst, shift in ((sinT, 0.0), (cosT, 0.25)):
                    if shift != 0.0:
                        B.ts(yb_, ya, shift, ALU.add, reads=[yab], writes=[ybb])
                    else:
                        B.cp(yb_, ya, reads=[yab], writes=[ybb])
                    B.cp(posi[:], yb_, reads=[ybb], writes=[posb])
                    B.cp(dst, posi[:], reads=[posb], writes=[csb])
                    B.tt(yb_, yb_, dst, ALU.subtract, reads=[ybb, csb], writes=[ybb])
                    B.stt(dst, yb_, 0.5, yb_, ALU.is_gt, ALU.subtract, reads=[ybb], writes=[csb])
                    B.stt(yb_, dst, 0.5, dst, ALU.is_gt, ALU.subtract, reads=[csb], writes=[ybb])
                    B.act(dst, yb_, AF.Sin, reads=[ybb], writes=[csb], scale=float(2.0 * math.pi))
                B.barrier()
            Q2 = B.sb(scope, [128, 2, T], BF16, "Q2")
            K2 = B.sb(scope, [128, 2, T], BF16, "K2")
            V2 = B.sb(scope, [128, NT, 2, 128], BF16, "V2")
            Q2b, K2b, V2b = Buf(), Buf(), Buf()
            PTr = Rot([(B.sb(scope, [128, 512], BF16, "PT"), Buf()) for _ in range(6)])
            SPall = B.sb(scope, [128, 10, 512], BF16, "SPall")
            SPr = Rot([(SPall[:, i, :], Buf()) for i in range(10)])
            VD2 = SPall[:, 0:8, :].rearrange("p a (t h d) -> p (a t) h d", h=2, d=128)
            VD2b = Buf()
            FPl = [(B.sb(scope, [128, 512], F32, "FP"), Buf()) for _ in range(14)]
            Er = Rot(FPl[0:10])
            Gr = Rot(FPl[10:14])
            QSr = Rot(FPl[0:6])
            SQr = Rot(FPl[6:12])
            TBr = Rot(FPl[12:14])
            GPr = Rot([(B.sb(scope, [128, 512], F32, "GP"), Buf()) for _ in range(2)])
            HB = B.sb(scope, [128, 3, 512], BF16, "HB")
            HBb = Buf()
            wf = B.sb(scope, [128, KC, 8], BF16, "wf")
            wfb = Buf()
            wfr = B.sb(scope, [128, KC, 128], BF16, "wfr")
            wfrb = Buf()
            bfb = B.sb(scope, [128, 8], F32, "bfb")
            negb = B.sb(scope, [128, 1], F32, "negb")
            bfbb, negbb = Buf(), Buf()
            gv = B.sb(scope, [128, 4], F32, "gv")
            gvb = Buf()
            rdr = Rot([(B.sb(scope, [128, 512], F32, "rd"), Buf()) for _ in range(2)])

            B.dma("pool", wf[:], wl.rearrange("(kc p) n -> p kc n", p=128)[:, :, O_FF:O_FF + 8], ds_small_pool, writes=[wfb])
            B.dma("sp", bfb[:], b_forget_d[l:l + 1, :].partition_broadcast(128), ds_small, writes=[bfbb])
            for ci, gd in enumerate((qnf_d, knf_d, qnd_d, knd_d)):
                for hh in range(2):
                    B.dma("sp", gv[hh * 64:(hh + 1) * 64, ci:ci + 1], gd[l:l + 1, :].rearrange("o d -> d o"), ds_small, writes=[gvb])
            B.ts(gv[:, 0:1], gv[:, 0:1], 0.125, ALU.mult, reads=[gvb], writes=[gvb])
            B.ts(gv[:, 2:3], gv[:, 2:3], 0.125, ALU.mult, reads=[gvb], writes=[gvb])
            B.memset(V2[:, :, :, 64:128], 1.0, writes=[V2b])

            bankS = Rot([PS[0], PS[1]])
            bankX = Rot([PS[6], PS[7]])
            bankXA = Rot([PS[0], PS[1], PS[2]])
            bankXB = Rot([PS[3], PS[4], PS[5]])
            bankS4 = Rot([PS[0], PS[1], PS[6], PS[7]])

            def warm(n):
                for _ in range(n):
                    B.mm(PS[7][0][:], ident, ones_b, True, True, reads=[cbb], writes=[PS[7][1]], signal=False)
            bankO4 = Rot([PS[4], PS[5], PS[6]])

            wplan = []
            for j in range(4):
                wplan.append((O_SQ + 128 * j, O_SK + 128 * j, O_SV + 128 * j))
            for j in range(4):
                wplan.append((O_FQ + 128 * j, O_FK + 128 * j, O_FV + 128 * j))
            for jp in range(2):
                for gi in range(3):
                    hh0 = gi * 4 + 2 * jp
                    wplan.append((O_DQ + 64 * hh0, O_DK + 64 * hh0, O_DV + 64 * hh0))
            wstate = {"next": 1, "q": [_w0]}

            def w_prefetch():
                i = wstate["next"]
                if i < len(wplan):
                    W3, W3b = W3r.next()
                    load_w3(W3, W3b, wl, *wplan[i])
                    wstate["q"].append((W3, W3b))
                    wstate["next"] = i + 1

            def w_get():
                if not wstate["q"]:
                    w_prefetch()
                return wstate["q"].pop(0)

            Q2B = MU[:, 0:2, :]
            K2B = MU[:, 2:4, :]
            V2B = OT[:, 8:10, :].rearrange("p a (t h d) -> p (a t) h d", h=2, d=128)
            setA = dict(Q=Q2, K=K2, V=V2, Qb=[Q2b], Kb=[K2b], Vb=V2b)
            setB = dict(Q=Q2B, K=K2B, V=V2B, Qb=[Buf()], Kb=[Buf()], Vb=Buf())

            def drain(gen):
                for _ in gen:
                    pass

            def run_interleaved(genA, nA, genB, nB):
                credit = 0.0
                ratio = (nB / max(nA, 1)) if genB is not None else 0.0
                for _ in genA:
                    credit += ratio
                    while genB is not None and credit >= 1.0:
                        credit -= 1.0
                        try:
                            next(genB)
                        except StopIteration:
                            genB = None
                if genB is not None:
                    drain(genB)

            def proj_v_gen(banks, W3, W3b, Vt, Vb, tile_cols):
                for t4 in range(0, NT, 4):
                    p, pb = banks.next()
                    for tt_ in range(4):
                        t = t4 + tt_
                        for kc in range(KC):
                            B.mm(p[:, tt_ * 128:(tt_ + 1) * 128], hT[:, kc, tile_cols(t)], W3[:, kc, 256:384], kc == 0, kc == KC - 1,
                                 reads=[W3b] + hTb, writes=[pb], signal=(kc == KC - 1 and tt_ == 3))
                    B.cp(Vt[:, t4:t4 + 4, :, 0:64], p[:].rearrange("p (t h d) -> p t h d", t=4, h=2), reads=[pb], writes=[Vb])
                    yield

            sbQb = [Buf(), Buf()]
            sbKb = [Buf(), Buf()]
            sbXP = Rot([PS[5]])

            def sb_prep_gen(j, W3, W3b, banks=None):
                banks = banks or sbXP
                slot = j % 2
                Vt, Vb = (V2, V2b) if slot == 0 else (V2B, setB["Vb"])
                items = []
                for g in range(NG):
                    items.append(dict(g=g, c0=0, X=Q2, Xb=sbQb[slot], sc=0.125))
                    items.append(dict(g=g, c0=128, X=K2, Xb=sbKb[slot], sc=None))

                def p0(t):
                    t["bk"] = banks.next()
                    proj_fm(t["bk"], W3, t["c0"], 128, t["g"], [W3b], hT, hTb)

                def p1(t):
                    g = t["g"]
                    if t["sc"] is None:
                        B.cp(t["X"][:, slot, g * 512:(g + 1) * 512], t["bk"][0][:], reads=[t["bk"][1]], writes=[t["Xb"]])
                    else:
                        B.ts(t["X"][:, slot, g * 512:(g + 1) * 512], t["bk"][0][:], t["sc"], ALU.mult, reads=[t["bk"][1]], writes=[t["Xb"]])
                yield from pipeline_gen(items, [p0, p1])
                yield from proj_v_gen(banks, W3, W3b, Vt, Vb, lambda t: slice(t * 128, (t + 1) * 128))
            SB_PREP_STEPS = 9 + 4

            def sb_attn_gen(j):
                if True:
                    slot = j % 2
                    Vt, Vb = (V2, V2b) if slot == 0 else (V2B, setB["Vb"])
                    Qb_, Kb_ = sbQb[slot], sbKb[slot]
                    tiles = []
                    for g in range(NG):
                        nk = 4 * g + 4
                        Ob = PS[4]
                        for idx, kt in enumerate(range(nk - 1, -1, -1)):
                            jd = kt - 4 * g
                            c0 = 128 * jd if jd > 0 else 0
                            tiles.append(dict(g=g, kt=kt, idx=idx, Ob=Ob, jd=jd, c0=c0, cs=slice(c0, 512)))

                    def a0(t):
                        g, kt, c0, cs = t["g"], t["kt"], t["c0"], t["cs"]
                        qs = slice(g * 512 + c0, (g + 1) * 512)
                        t["z"] = []
                        for sl in range(2):
                            pb0 = 64 * sl
                            zb = bankS4.next()
                            B.mm(zb[0][:, cs], K2[pb0:pb0 + 64, slot, kt * 128:(kt + 1) * 128], Q2[pb0:pb0 + 64, slot, qs], True, True,
                                 reads=[Kb_, Qb_], writes=[zb[1]])
                            t["z"].append(zb)

                    def a1(t):
                        cs, c0 = t["cs"], t["c0"]
                        t["E"], t["S"] = [], []
                        for sl in range(2):
                            zb = t["z"][sl]
                            Et, Eb = Er.next()
                            B.act(Et[:, cs], zb[0][:, cs], AF.Exp, reads=[zb[1]], writes=[Eb])
                            t["E"].append((Et, Eb))
                        for sl in range(2):
                            Et, Eb = t["E"][sl]
                            St, Sb = SPr.next()
                            B.act(St[:, cs], Et[:, cs], AF.Ln, reads=[Eb], writes=[Sb], bias=1.0)
                            if t["jd"] >= 0:
                                B.tt(St[:, c0:c0 + 128], St[:, c0:c0 + 128], mgt, ALU.mult, reads=[Sb, cbb], writes=[Sb])
                            t["S"].append((St, Sb))

                    def a2(t):
                        cs = t["cs"]
                        for sl in range(2):
                            Cb = PS[2 + sl]
                            St, Sb = t["S"][sl]
                            B.mm(Cb[0][:, cs], triI, St[:, cs], t["idx"] == 0, False, reads=[Sb, cbb], writes=[Cb[1]], signal=True)

                    def a3(t):
                        cs = t["cs"]
                        t["G"] = []
                        for sl in range(2):
                            Cb = PS[2 + sl]
                            Gt, Gb = Gr.next()
                            B.act(Gt[:, cs], Cb[0][:, cs], AF.Exp, reads=[Cb[1]], writes=[Gb], scale=-1.0)
                            t["G"].append((Gt, Gb))
                        for sl in range(2):
                            Cb = PS[2 + sl]
                            St, Sb = t["S"][sl]
                            if t["kt"] > 0:
                                B.mm(Cb[0][:, cs], triC, St[:, cs], False, False, reads=[Sb, cbb], writes=[Cb[1]], signal=True)

                    def a4(t):
                        cs, c0 = t["cs"], t["c0"]
                        t["P"] = []
                        for sl in range(2):
                            Et, Eb = t["E"][sl]
                            Gt, Gb = t["G"][sl]
                            Pt, Pb = PTr.next()
                            B.tt(Pt[:, cs], Et[:, cs], Gt[:, cs], ALU.mult, reads=[Eb, Gb], writes=[Pb])
                            if t["jd"] >= 0:
                                B.tt(Pt[:, c0:c0 + 128], Pt[:, c0:c0 + 128], mgt, ALU.mult, reads=[Pb, cbb], writes=[Pb])
                            t["P"].append((Pt, Pb))

                    def a5(t):
                        g, kt, cs, Ob = t["g"], t["kt"], t["cs"], t["Ob"]
                        for sl in range(2):
                            Pt, Pb = t["P"][sl]
                            pb0 = 64 * sl
                            B.mm(Ob[0][pb0:pb0 + 64, cs], Vt[:, kt, sl, 0:64], Pt[:, cs], t["idx"] == 0, kt == 0, reads=[Vb, Pb], writes=[Ob[1]], signal=True)
                        if kt == 0:
                            B.cp(OT[:, 4 + j, g * 512:(g + 1) * 512], Ob[0][:, :], reads=[Ob[1]], writes=[OTb[4 + j]])
                    yield from pipeline_gen(tiles, [a0, a1, a2, a3, a4, a5])

            SB_ATTN_STEPS = 40 + 5

            def sb_branch():
                W = [None] * 4
                W[0] = w_get()
                w_prefetch()
                drain(sb_prep_gen(0, W[0][0], W[0][1], Rot([PS[4], PS[5], PS[6], PS[7]])))
                for j in range(4):
                    if j + 1 < 4:
                        W[j + 1] = w_get()
                        gB = sb_prep_gen(j + 1, *W[j + 1])
                    else:
                        gB = None
                    if j + 2 < 4:
                        w_prefetch()
                    run_interleaved(sb_attn_gen(j), SB_ATTN_STEPS, gB, SB_PREP_STEPS)
                w_prefetch()

            def n_proj(t):
                t["bk"] = bankXA.next()
                proj_fm(t["bk"], t["W"], t["c0"], 128, t["g"], [t["Wb"]], hT, hTb)

            def n_sq(t):
                p, pb = t["bk"]
                qs_, qsb = QSr.next()
                B.act(qs_[:], p[:], AF.Copy, reads=[pb], writes=[qsb])
                sq, sqb = SQr.next()
                B.tt(sq[:], p[:], qs_[:], ALU.mult, reads=[pb, qsb], writes=[sqb])
                t["qs"], t["sq"] = (qs_, qsb), (sq, sqb)

            def n_blk(t, banks=None):
                sq, sqb = t["sq"]
                b2 = (banks or bankXB).next()
                B.mm(b2[0][:], blk, sq[:], True, True, reads=[sqb, cfb], writes=[b2[1]])
                t["b2"] = b2

            def n_rstd(t):
                b2 = t["b2"]
                r1, r1b = t["sq"]
                qs_, qsb = t["qs"]
                B.act(r1[:], b2[0][:], AF.Ln, reads=[b2[1]], writes=[r1b], scale=1.0 / 64.0, bias=EPS)
                B.act(r1[:], r1[:], AF.Exp, reads=[r1b], writes=[r1b], scale=-0.5)
                gcol = t["gcol"]
                B.stt(qs_[:], qs_[:], gv[:, gcol:gcol + 1], r1[:], ALU.mult, ALU.mult, reads=[qsb, gvb, r1b], writes=[qsb])

            def init_fox_set(S):
                Q, K = S["Q"], S["K"]
                B.memset(Q[:, :, :], 0.0, writes=S["Qb"], eng="dve")
                B.memset(K[:, :, :], 0.0, writes=S["Kb"], eng="dve")
                B.memset(Q[96:99, 0, :], 1.0, writes=S["Qb"], eng="dve")
                B.memset(Q[32:35, 1, :], 1.0, writes=S["Qb"], eng="dve")
                B.memset(K[64:67, 0, :], -1.0, writes=S["Kb"], eng="dve")
                B.memset(K[0:3, 1, :], -1.0, writes=S["Kb"], eng="dve")

            fbXA = Rot([PS[4]])
            fbXB = Rot([PS[5]])
            fbV = Rot([PS[5]])
            bankS3 = Rot([PS[0], PS[1], PS[6]])
            foxO = Rot([PS[2], PS[3], PS[7]])

            def fox_prep_gen(j, S, W3, W3b, xa=None, xb=None, xv=None):
                xa, xb, xv = xa or fbXA, xb or fbXB, xv or fbV
                ha, hb_ = 2 * j, 2 * j + 1
                Q, K, Qb, Kb = S["Q"], S["K"], S["Qb"], S["Kb"]
                for (r0, hh) in ((64, ha), (96, ha), (0, hb_), (32, hb_)):
                    B.cp(wfr[:, :, r0:r0 + 3], wf[:, :, hh:hh + 1].to_broadcast([128, KC, 3]), reads=[wfb], writes=[wfrb])
                B.ts(negb[64:128, :], bfb[64:128, ha:ha + 1], -1.0, ALU.mult, reads=[bfbb], writes=[negbb])
                B.ts(negb[0:64, :], bfb[0:64, hb_:hb_ + 1], -1.0, ALU.mult, reads=[bfbb], writes=[negbb])
                items = []
                for g in range(NG):
                    items.append(dict(kind="q", g=g, W=W3, Wb=W3b, c0=0, gcol=0, X=Q, Xb=Qb))
                    items.append(dict(kind="k", g=g, W=W3, Wb=W3b, c0=128, gcol=1, X=K, Xb=Kb))
                    items.append(dict(kind="f", g=g, W=wfr, Wb=wfrb, c0=0))
                gstate = {"prev": None}

                def f0(t):
                    t["bk"] = xa.next()
                    proj_fm(t["bk"], t["W"], t["c0"], 128, t["g"], [t["Wb"]], hT, hTb)

                def f1(t):
                    if t["kind"] != "f":
                        return n_sq(t)
                    bk = t["bk"]
                    e1, e1b = QSr.next()
                    B.act(e1[:], bk[0][:], AF.Exp, reads=[bk[1], negbb], writes=[e1b], scale=-1.0, bias=negb[:, 0:1])
                    B.act(e1[:], e1[:], AF.Ln, reads=[e1b], writes=[e1b], bias=1.0)
                    t["e1"] = (e1, e1b)

                def f2(t):
                    if t["kind"] != "f":
                        return n_blk(t, xb)
                    e1, e1b = t["e1"]
                    Gt, Gb = GPr.next()
                    Gprev = gstate["prev"]
                    init = 0.0 if Gprev is None else Gprev[0][:, 511:512]
                    rds = [e1b, cbb] + ([] if Gprev is None else [Gprev[1]])
                    B.op("dve", lambda: nc.vector.tensor_tensor_scan(
                        out=Gt[:], data0=ones_b, data1=e1[:], initial=init, op0=ALU.mult, op1=ALU.add), reads=rds, writes=[Gb])
                    gstate["prev"] = (Gt, Gb)
                    t["G"] = (Gt, Gb)

                def f3(t):
                    gs = slice(t["g"] * 512, (t["g"] + 1) * 512)
                    if t["kind"] != "f":
                        b2 = t["b2"]
                        r1, r1b = t["sq"]
                        qs_, qsb = t["qs"]
                        X, Xb = t["X"], t["Xb"]
                        gcol = t["gcol"]
                        B.act(r1[:], b2[0][:], AF.Ln, reads=[b2[1]], writes=[r1b], scale=1.0 / 64.0, bias=EPS)
                        B.act(r1[:], r1[:], AF.Exp, reads=[r1b], writes=[r1b], scale=-0.5)
                        B.stt(X[0:64, 0, gs], qs_[0:64, :], gv[0:64, gcol:gcol + 1], r1[0:64, :], ALU.mult, ALU.mult,
                              reads=[qsb, gvb, r1b], writes=Xb)
                        B.stt(X[64:128, 1, gs], qs_[64:128, :], gv[64:128, gcol:gcol + 1], r1[64:128, :], ALU.mult, ALU.mult,
                              reads=[qsb, gvb, r1b], writes=Xb)
                        return
                    Gt, Gb = t["G"]
                    t1, t1b = t["e1"]
                    B.cp(HB[:, 0, :], Gt[:], reads=[Gb], writes=[HBb])
                    B.stt(t1[:], HB[:, 0, :], negc, Gt[:], ALU.mult, ALU.add, reads=[HBb, Gb, cfb], writes=[t1b])
                    B.cp(HB[:, 1, :], t1[:], reads=[t1b], writes=[HBb])
                    B.stt(t1[:], HB[:, 1, :], negd, t1[:], ALU.mult, ALU.add, reads=[HBb, t1b, cfb], writes=[t1b])
                    B.cp(Q[64:67, 0, gs], t1[64:67, :], reads=[t1b], writes=Qb)
                    B.cp(K[96:99, 0, gs], t1[96:99, :], reads=[t1b], writes=Kb)
                    B.cp(Q[0:3, 1, gs], t1[0:3, :], reads=[t1b], writes=Qb)
                    B.cp(K[32:35, 1, gs], t1[32:35, :], reads=[t1b], writes=Kb)
                yield from pipeline_gen(items, [f0, f1, f2, f3])
                yield from proj_v_gen(xv, W3, W3b, S["V"], S["Vb"], lambda t: slice(t * 128, (t + 1) * 128))
            FOX_PREP_STEPS = 12 + 3 + 4

            def fox_attn_gen(j, S):
                Q, K, V, Qb, Kb, Vb = S["Q"], S["K"], S["V"], S["Qb"], S["Kb"], S["Vb"]
                tiles = []
                for g in range(NG):
                    nk = 4 * g + 4
                    chains = []
                    for sl in range(2):
                        Ob = foxO.next()
                        ch = []
                        for kt in range(nk):
                            jd = kt - 4 * g
                            c0 = 128 * jd if jd > 0 else 0
                            ch.append(dict(sl=sl, g=g, kt=kt, nk=nk, Ob=Ob, jd=jd, c0=c0, cs=slice(c0, 512)))
                        chains.append(ch)
                    for a, b_ in zip(chains[0], chains[1]):
                        tiles.append(a)
                        tiles.append(b_)

                def a0(t):
                    g, kt, sl, c0, cs = t["g"], t["kt"], t["sl"], t["c0"], t["cs"]
                    qs = slice(g * 512 + c0, (g + 1) * 512)
                    zb = bankS3.next()
                    for _ in range(DUP_FOX):
                        B.mm(zb[0][:, cs], K[:, sl, kt * 128:(kt + 1) * 128], Q[:, sl, qs], True, True,
                             reads=Kb + Qb, writes=[zb[1]], signal=False)
                    B.mm(zb[0][:, cs], K[:, sl, kt * 128:(kt + 1) * 128], Q[:, sl, qs], True, True,
                         reads=Kb + Qb, writes=[zb[1]])
                    t["z"] = zb

                def a1(t):
                    cs, c0, zb = t["cs"], t["c0"], t["z"]
                    Pt, Pb = PTr.next()
                    B.act(Pt[:, cs], zb[0][:, cs], AF.Exp, reads=[zb[1]], writes=[Pb])
                    if t["jd"] >= 0:
                        B.tt(Pt[:, c0:c0 + 128], Pt[:, c0:c0 + 128], mge, ALU.mult, reads=[Pb, cbb], writes=[Pb])
                    t["P"] = (Pt, Pb)

                def a2(t):
                    g, kt, sl, nk, cs, Ob = t["g"], t["kt"], t["sl"], t["nk"], t["cs"], t["Ob"]
                    Pt, Pb = t["P"]
                    pb0 = 64 * sl
                    B.mm(Ob[0][:, cs], V[:, kt, sl, :], Pt[:, cs], kt == 0, kt == nk - 1, reads=[Vb, Pb], writes=[Ob[1]], signal=True)
                    if kt == nk - 1:
                        rd, rdb = rdr.next()
                        B.act(rd[0:64, :], Ob[0][64:128, :], AF.Ln, reads=[Ob[1]], writes=[rdb])
                        B.act(rd[0:64, :], rd[0:64, :], AF.Exp, reads=[rdb], writes=[rdb], scale=-1.0)
                        B.tt(OT[pb0:pb0 + 64, j, g * 512:(g + 1) * 512], Ob[0][0:64, :], rd[0:64, :], ALU.mult, reads=[Ob[1], rdb], writes=[OTb[j]])
                yield from pipeline_gen(tiles, [a0, a1, a2], lag=2)
            FOX_ATTN_STEPS = 80 + 4

            def fox_branch():
                B.memset(wfr[:], 0.0, writes=[wfrb], eng="dve")
                init_fox_set(setA)
                sets = [setB, setA, setB, setA]
                W = [None] * 4
                W[0] = w_get()
                w_prefetch()
                drain(fox_prep_gen(0, sets[0], W[0][0], W[0][1], Rot([PS[0], PS[1], PS[2]]), Rot([PS[3], PS[4], PS[5]]), Rot([PS[6], PS[7]])))
                for j in range(4):
                    if j + 1 < 4:
                        W[j + 1] = w_get()
                        gB = fox_prep_gen(j + 1, sets[j + 1], *W[j + 1])
                    else:
                        gB = None
                    if j + 2 < 4:
                        w_prefetch()
                    run_interleaved(fox_attn_gen(j, sets[j]), FOX_ATTN_STEPS, gB, FOX_PREP_STEPS)
                w_prefetch()

            dXA = Rot([PS[4]])
            dXB = Rot([PS[5], PS[7]])
            Qs_b = [Buf(), Buf()]
            Ks_b = [Buf(), Buf()]

            def dil_prep_gen(u, W3, W3b, xa=None, xb=None):
                xa, xb = xa or dXA, xb or dXB
                jp, gi = u // 3, u % 3
                r = DILS[gi]
                slot = u % 2
                items = []
                for g in range(NG):
                    items.append(dict(g=g, W=W3, Wb=W3b, c0=0, gcol=2, X=Q2, Xb=Qs_b[slot]))
                    items.append(dict(g=g, W=W3, Wb=W3b, c0=128, gcol=3, X=K2, Xb=Ks_b[slot]))

                def d0(t):
                    t["bk"] = xa.next()
                    proj_fm(t["bk"], t["W"], t["c0"], 128, t["g"], [t["Wb"]], hT, hTb)

                def d2(t):
                    return n_blk(t, xb)

                def d4(t):
                    qn, qnb = t["qs"]
                    gs = slice(t["g"] * 512, (t["g"] + 1) * 512)
                    b2 = xb.next()
                    B.mm(b2[0][:], perm, qn[:], True, True, reads=[qnb, cfb], writes=[b2[1]])
                    t["b3"] = b2
                    ta, tab = t["sq"]
                    B.tt(ta[:], qn[:], cosT[:, gs], ALU.mult, reads=[qnb, csb], writes=[tab], eng="pool")

                def d5(t):
                    g = t["g"]
                    gs = slice(g * 512, (g + 1) * 512)
                    b2 = t["b3"]
                    ta, tab = t["sq"]
                    X, Xb = t["X"], t["Xb"]
                    tb_, tbb = TBr.next()
                    B.tt(tb_[:], b2[0][:], sinT[:, gs], ALU.mult, reads=[b2[1], csb], writes=[tbb])
                    if r == 1:
                        B.tt(X[:, slot, gs], ta[:], tb_[:], ALU.add, reads=[tab, tbb], writes=[Xb], eng="pool")
                    else:
                        n0 = g * 512 // r
                        dst = X[:, slot, :].rearrange("p (c n) -> p n c", c=r)[:, n0:n0 + 512 // r, :]
                        B.tt(dst, ta[:].rearrange("p (n c) -> p n c", c=r), tb_[:].rearrange("p (n c) -> p n c", c=r), ALU.add,
                             reads=[tab, tbb], writes=[Xb], eng="pool")
                yield from pipeline_gen(items, [d0, n_sq, d2, n_rstd, d4, d5])
                ntl = (T // r) // 128

                def tcols(t):
                    c, i = t // ntl, t % ntl
                    st = c + r * 128 * i
                    return slice(st, st + r * 127 + 1, r)
                Vt, Vb = (V2, V2b) if u % 2 == 0 else (VD2, VD2b)
                yield from proj_v_gen(xa, W3, W3b, Vt, Vb, tcols)
            DIL_PREP_STEPS = 8 + 5 + 4

            def dil_attn_gen(u):
                jp, gi = u // 3, u % 3
                r = DILS[gi]
                ntl = (T // r) // 128
                slot = u % 2
                Qb, Kb = Qs_b[slot], Ks_b[slot]
                Vt, Vb = (V2, V2b) if u % 2 == 0 else (VD2, VD2b)
                tiles = []
                obs = {}
                flip = 0
                for t4 in range(0, NT, 4):
                    obs[t4] = [PS[2], PS[3]]
                started = set()
                for tk in range(NT):
                    i = tk % ntl
                    nq = 2 if (i + 1 < ntl) else 1
                    for sl in range(2):
                        outs = []
                        for qq in range(nq):
                            tq = tk + qq
                            t4 = (tq // 4) * 4
                            key = (t4, sl)
                            outs.append(dict(t4=t4, col=(tq % 4) * 128, pcol=qq * 128, start=(key not in started)))
                            started.add(key)
                        if nq == 2 and outs[0]["t4"] == outs[1]["t4"]:
                            outs = [dict(t4=outs[0]["t4"], col=outs[0]["col"], pcol=0, start=outs[0]["start"], n=256)]
                        else:
                            for o in outs:
                                o["n"] = 128
                        fin = [o["t4"] for o in outs if tk == o["t4"] + 3]
                        tiles.append(dict(sl=sl, tk=tk, nq=nq, outs=outs, fin=fin))
                tiles2 = []
                for t in tiles:
                    spill = [o for o in t["outs"] if o["t4"] > (t["tk"] // 4) * 4]
                    keep = [o for o in t["outs"] if o["t4"] <= (t["tk"] // 4) * 4]
                    if spill and keep:
                        t2 = dict(t)
                        t2["outs"], t2["fin"], t2["reuse"] = spill, [], t
                        t["outs"] = keep
                        tiles2.append(t)
                        tiles2.append(t2)
                    else:
                        tiles2.append(t)
                tiles = tiles2

                def a0(t):
                    if "reuse" in t:
                        return
                    pb0 = 64 * t["sl"]
                    tk, n = t["tk"], 128 * t["nq"]
                    zb = bankS3.next()
                    B.mm(zb[0][:, 0:n], K2[pb0:pb0 + 64, slot, tk * 128:(tk + 1) * 128], Q2[pb0:pb0 + 64, slot, tk * 128:tk * 128 + n],
                         True, True, reads=[Kb, Qb], writes=[zb[1]])
                    t["z"] = zb

                def a1(t):
                    if "reuse" in t:
                        t["P"] = t["reuse"]["P"]
                        return
                    zb = t["z"]
                    n = 128 * t["nq"]
                    Pt, Pb = PTr.next()
                    B.act(Pt[:, 0:n], zb[0][:, 0:n], AF.Exp, reads=[zb[1]], writes=[Pb])
                    B.tt(Pt[:, 0:n], Pt[:, 0:n], mband[:, 0:n], ALU.mult, reads=[Pb, cbb], writes=[Pb])
                    t["P"] = (Pt, Pb)

                def a2(t):
                    sl = t["sl"]
                    Pt, Pb = t["P"]
                    for o in t["outs"]:
                        Ob = obs[o["t4"]][sl]
                        B.mm(Ob[0][:, o["col"]:o["col"] + o["n"]], Vt[:, t["tk"], sl, :], Pt[:, o["pcol"]:o["pcol"] + o["n"]], o["start"], True,
                             reads=[Vb, Pb], writes=[Ob[1]], signal=True)
                    for t4 in t["fin"]:
                        Ob = obs[t4][sl]
                        if r == 1:
                            av = acc[:, sl, t4 * 128:(t4 + 4) * 128]
                            src = Ob[0][:]
                        elif r == 4:
                            c = t4 // ntl
                            av = acc[:, sl, c:c + 4 * 511 + 1:4]
                            src = Ob[0][:]
                        else:
                            av = acc[:, sl, :].rearrange("p (n c) -> p c n", c=16)[:, t4:t4 + 4, :]
                            src = Ob[0][:].rearrange("p (c n) -> p c n", c=4)
                        if gi == 0:
                            B.cp(av, src, reads=[Ob[1]], writes=[accb[sl]])
                        else:
                            B.tt(av, src, av, ALU.add, reads=[Ob[1], accb[sl]], writes=[accb[sl]])
                yield from pipeline_gen(tiles, [a0, a1, a2], lag=2)
            DIL_ATTN_STEPS = 42

            def dil_branch():
                B.memset(VD2[:, :, :, 64:128], 1.0, writes=[VD2b])
                Wd = [None] * 6
                Wd[0] = w_get()
                w_prefetch()
                drain(dil_prep_gen(0, Wd[0][0], Wd[0][1], Rot([PS[0], PS[1], PS[2]]), Rot([PS[3], PS[4], PS[5], PS[6]])))
                for u in range(6):
                    jp, gi = u // 3, u % 3
                    r = DILS[gi]
                    ntl = (T // r) // 128

                    if u + 1 < 6:
                        Wd[u + 1] = w_get()
                        gB = dil_prep_gen(u + 1, *Wd[u + 1])
                    else:
                        gB = None
                    w_prefetch()
                    run_interleaved(dil_attn_gen(u), DIL_ATTN_STEPS, gB, DIL_PREP_STEPS)
                    if gi == 2:
                        for sl in range(2):
                            pb0 = 64 * sl
                            for g in range(NG):
                                gs = slice(g * 512, (g + 1) * 512)
                                rd, rdb = rdr.next()
                                B.act(rd[0:64, :], acc[64:128, sl, gs], AF.Ln, reads=[accb[sl]], writes=[rdb])
                                B.act(rd[0:64, :], rd[0:64, :], AF.Exp, reads=[rdb], writes=[rdb], scale=-1.0)
                                B.tt(OT[pb0:pb0 + 64, 8 + jp, gs], acc[0:64, sl, gs], rd[0:64, :], ALU.mult, reads=[accb[sl], rdb], writes=[OTb[8 + jp]])

            B.memset(V2B[:, :, :, 64:128], 1.0, writes=[setB["Vb"]])
            init_fox_set(setB)
            sb_branch()
            B.barrier()
            fox_branch()
            B.barrier()
            dil_branch()

        def merge_phase(scope, l, hT, hTb, OT, OTb, MU, MUb):
            wl = w_in_d[l]
            Wgr = Rot([(B.sb(scope, [128, KC, 384], BF16, "Wg"), Buf()) for _ in range(2)])
            Wur = Rot([(B.sb(scope, [128, 10, 128], BF16, "Wu"), Buf()) for _ in range(2)])
            gsr = Rot([(B.sb(scope, [128, 512], F32, "gs"), Buf()) for _ in range(4)])
            tr = Rot([(B.sb(scope, [128, 512], F32, "tm"), Buf()) for _ in range(4)])
            bankG = Rot([PS[0], PS[1], PS[2], PS[3]])
            bankU = Rot([PS[4], PS[5], PS[6], PS[7]])
            src = wl.rearrange("(kc p) n -> p kc n", p=128)

            def load(fc):
                Wg, Wgb = Wgr.next()
                Wu, Wub = Wur.next()
                ds = wsem()
                for b_ in range(3):
                    c0 = O_G + b_ * 1024 + fc * 128
                    B.dma("pool", Wg[:, :, b_ * 128:(b_ + 1) * 128], src[:, :, c0:c0 + 128], ds, writes=[Wgb])
                ds = wsem()
                B.dma("pool", Wu[:, 0:4, :], wuf_d[l].rearrange("(kc p) n -> p kc n", p=128)[:, :, fc * 128:(fc + 1) * 128], ds, writes=[Wub])
                B.dma("pool", Wu[:, 4:8, :], wus_d[l].rearrange("(kc p) n -> p kc n", p=128)[:, :, fc * 128:(fc + 1) * 128], ds, writes=[Wub])
                B.dma("pool", Wu[:, 8:10, :], wud_d[l].rearrange("(kc p) n -> p kc n", p=128)[:, :, fc * 128:(fc + 1) * 128], ds, writes=[Wub])
                return (Wg, Wgb, Wu, Wub)
            nxt = load(0)
            for fc in range(8):
                Wg, Wgb, Wu, Wub = nxt
                if fc + 1 < 8:
                    nxt = load(fc + 1)
                for g in range(NG):
                    gs = slice(g * 512, (g + 1) * 512)
                    prods = []
                    for b_, (k0, k1) in enumerate(((0, 4), (4, 8), (8, 10))):
                        bg = bankG.next()
                        proj_fm(bg, Wg, b_ * 128, 128, g, [Wgb], hT, hTb)
                        gsig, gsb = gsr.next()
                        B.act(gsig[:], bg[0][:], AF.Sigmoid, reads=[bg[1]], writes=[gsb])
                        p, pb = bankU.next()
                        for kc in range(k0, k1):
                            B.mm(p[:], Wu[:, kc, :], OT[:, kc, gs], kc == k0, kc == k1 - 1, reads=[Wub, OTb[kc]], writes=[pb])
                        tm, tmb = tr.next()
                        B.tt(tm[:], p[:], gsig[:], ALU.mult, reads=[pb, gsb], writes=[tmb])
                        prods.append((tm, tmb))
                    B.tt(prods[0][0][:], prods[0][0][:], prods[1][0][:], ALU.add, reads=[prods[0][1], prods[1][1]], writes=[prods[0][1]])
                    B.tt(MU[:, fc, gs], prods[0][0][:], prods[2][0][:], ALU.add, reads=[prods[0][1], prods[2][1]], writes=[MUb[fc]])

        def outproj_phase(scope, l, x_src, MU, MUb, xs, xsb):
            wo = B.sb(scope, [128, KC, D], BF16, "wo")
            wob = Buf()
            ds = wsem()
            src = wout_d[l].rearrange("(kc p) n -> p kc n", p=128)
            for hf in range(2):
                B.dma("pool", wo[:, :, hf * 512:(hf + 1) * 512], src[:, :, hf * 512:(hf + 1) * 512], ds, writes=[wob])
            xin = Rot([(B.sb(scope, [128, D], F32, "xin"), Buf(), ds_x[i]) for i in range(2)])
            banks = Rot([PS[0], PS[1], PS[2], PS[3]])
            for t in range(NT):
                xi, xib, dsx = xin.next()
                B.dma("sp", xi[:], x_src[t * 128:(t + 1) * 128, :], dsx, writes=[xib])
                for hf in range(2):
                    p, pb = banks.next()
                    for kc in range(KC):
                        B.mm(p[:], MU[:, kc, t * 128:(t + 1) * 128], wo[:, kc, hf * 512:(hf + 1) * 512], kc == 0, kc == KC - 1,
                             reads=[MUb[kc], wob], writes=[pb])
                    B.tt(xs[:, t, hf * 512:(hf + 1) * 512], p[:], xi[:, hf * 512:(hf + 1) * 512], ALU.add, reads=[pb, xib], writes=[xsb[t]])

        def mlp_phase(scope, l, hT, hTb, MU, MUb, xs, xsb, x_dst):
            w1r = Rot([(B.sb(scope, [128, KC, 512], BF16, "w1"), Buf()) for _ in range(2)])
            w2r = Rot([(B.sb(scope, [128, 4, D], BF16, "w2"), Buf()) for _ in range(2)])
            rl = Rot([(B.sb(scope, [128, 512], F32, "rl"), Buf()) for _ in range(2)])
            b1 = Rot([PS[0], PS[1], PS[2], PS[3]])
            b2 = Rot([PS[4], PS[5], PS[6], PS[7]])
            w1src = w1_d[l].rearrange("(kc p) n -> p kc n", p=128)
            w2src = w2_d[l].rearrange("(c p) n -> p c n", p=128)
            for fb in range(8):
                w1, w1b = w1r.next()
                w2, w2b = w2r.next()
                ds = wsem()
                B.dma("pool", w1[:], w1src[:, :, fb * 512:(fb + 1) * 512], ds, writes=[w1b])
                ds = wsem()
                B.dma("pool", w2[:], w2src[:, fb * 4:(fb + 1) * 4, :], ds, writes=[w2b])
                ub = (fb % 2) * 4
                for c in range(4):
                    for g in range(NG):
                        gs = slice(g * 512, (g + 1) * 512)
                        bk = b1.next()
                        proj_fm(bk, w1, c * 128, 128, g, [w1b], hT, hTb)
                        r_, rb = rl.next()
                        B.act(r_[:], bk[0][:], AF.Relu, reads=[bk[1]], writes=[rb])
                        B.stt(MU[:, ub + c, gs], bk[0][:], 0.0, r_[:], ALU.max, ALU.mult, reads=[bk[1], rb], writes=[MUb[ub + c]])
                for t in range(NT):
                    for hf in range(2):
                        p, pb = b2.next()
                        for c in range(4):
                            B.mm(p[:], MU[:, ub + c, t * 128:(t + 1) * 128], w2[:, c, hf * 512:(hf + 1) * 512], c == 0, c == 3,
                                 reads=[MUb[ub + c], w2b], writes=[pb])
                        xv = xs[:, t, hf * 512:(hf + 1) * 512]
                        B.tt(xv, p[:], xv, ALU.add, reads=[pb, xsb[t]], writes=[xsb[t]])
            for t in range(NT):
                B.dma("sp", x_dst[t * 128:(t + 1) * 128, :], xs[:, t, :], ds_outs[t % 4], reads=[xsb[t]])

        for s in range(nseq):
            for l in range(nlayer):
                x_src = x_d[s] if l == 0 else xmid_d[s]
                x_dst = out_d[s] if l == nlayer - 1 else xmid_d[s]
                with contextlib.ExitStack() as L:
                    hT = B.sb(L, [128, KC, T], BF16, "hT")
                    hTb = [Buf() for _ in range(NG)]
                    MU = B.sb(L, [128, KC, T], BF16, "MU")
                    MUb = [Buf() for _ in range(KC)]
                    with contextlib.ExitStack() as N1:
                        xin = Rot([(B.sb(N1, [128, D], F32, "xin"), Buf(), ds_x[i]) for i in range(4)])

                        def get_x(t):
                            xi, xib, dsx = xin.next()
                            B.dma("sp", xi[:], x_src[t * 128:(t + 1) * 128, :], dsx, writes=[xib])
                            return xi[:], xib
                        norm_phase(N1, get_x, attn_norm_d[l:l + 1, :], hT, hTb)
                        B.barrier()
                    with contextlib.ExitStack() as AB:
                        OT = B.sb(AB, [128, 10, T], BF16, "OT")
                        OTb = [Buf() for _ in range(10)]
                        with contextlib.ExitStack() as A:
                            attention_phase(A, l, s, hT, hTb, OT, OTb, MU)
                            B.barrier()
                        with contextlib.ExitStack() as Bm:
                            merge_phase(Bm, l, hT, hTb, OT, OTb, MU, MUb)
                            B.barrier()
                    with contextlib.ExitStack() as CD:
                        xs = B.sb(CD, [128, NT, D], F32, "xs")
                        xsb = [Buf() for _ in range(NT)]
                        with contextlib.ExitStack() as C:
                            outproj_phase(C, l, x_src, MU, MUb, xs, xsb)
                            B.barrier()
                        with contextlib.ExitStack() as Dm:
                            norm_phase(Dm, lambda t: (xs[:, t, :], xsb[t]), mlp_norm_d[l:l + 1, :], hT, hTb)
                            mlp_phase(Dm, l, hT, hTb, MU, MUb, xs, xsb, x_dst)
                            B.barrier()
        B.barrier()
    return nc


_CACHE = {}


def kernel(x, positions, attn_norm, w_in, b_forget, q_norm_fox, k_norm_fox, q_norm_dil, k_norm_dil,
           w_up_fox, w_up_sb, w_up_dil, w_out, mlp_norm, w_mlp_in, w_mlp_out):
    ncores = 8
    if "nc" not in _CACHE:
        _CACHE["nc"] = build_program()
    nc = _CACHE["nc"]
    cb, cf = host_consts()
    f = lambda a: np.ascontiguousarray(np.asarray(a, dtype=np.float32))
    shared = {
        "attn_norm": f(attn_norm), "w_in": f(w_in), "b_forget": f(b_forget),
        "q_norm_fox": f(q_norm_fox), "k_norm_fox": f(k_norm_fox), "q_norm_dil": f(q_norm_dil), "k_norm_dil": f(k_norm_dil),
        "w_up_fox": f(w_up_fox), "w_up_sb": f(w_up_sb), "w_up_dil": f(w_up_dil), "w_out": f(w_out),
        "mlp_norm": f(mlp_norm), "w_mlp_in": f(w_mlp_in), "w_mlp_out": f(w_mlp_out),
        "cst_b": cb, "cst_f": cf,
    }
    x = np.asarray(x, dtype=np.float32)
    positions = np.asarray(positions, dtype=np.int32)
    in_maps = []
    for c in range(ncores):
        m = dict(shared)
        m["x"] = np.ascontiguousarray(x[2 * c:2 * c + 2])
        m["positions"] = np.ascontiguousarray(positions[2 * c:2 * c + 2])
        in_maps.append(m)
    res = run_bass_kernel_spmd(nc, in_maps, core_ids=list(range(ncores)))
    out = np.concatenate([np.asarray(r["out"], dtype=np.float32) for r in res.results], axis=0)
    return out
```
